# Optimizing a Trainium2 kernel written in Bass

```python
import math
import jax, jax.numpy as jnp
from jax import lax
import numpy as np

D_MODEL = 1024
BATCH = 4
SEQ = 8192
DEPTH = 2

D_FF = 2816
ALPHA = (2 * DEPTH) ** 0.25
BETA = (8 * DEPTH) ** -0.25
LN_EPS = 1e-5
NEG = -1e30

A_HEADS = 8
A_DH = 64
A_WIDTH = A_HEADS * A_DH
MOBA_BLOCK = 256
MOBA_TOPK = 3
MOBA_QCHUNK = 64
ROPE_THETA = 10000.0

B_GROUPS = 8
B_DG = 64
B_WIDTH = B_GROUPS * B_DG
SGU_CHUNK = 128
AB_IN = 3 * A_WIDTH + 2 * B_WIDTH
AB_MIX = A_WIDTH + B_WIDTH

C_HEADS = 4
C_DQK = 128
C_DV = 256
C_QK_WIDTH = C_HEADS * C_DQK
C_WIDTH = C_HEADS * C_DV
C_IN = 2 * C_QK_WIDTH + 2 * C_WIDTH + 2 * C_HEADS
MLSTM_CHUNK = 64
CONV_W = 4

kernel_name = "hybrid_moba_sgu_mlstm_macaron_deepnorm"


def layer_norm(x, g, b):
    xf = x.astype(jnp.float32)
    mu = jnp.mean(xf, axis=-1, keepdims=True)
    var = jnp.mean(jnp.square(xf - mu), axis=-1, keepdims=True)
    y = (xf - mu) * lax.rsqrt(var + LN_EPS) * g.astype(jnp.float32) + b.astype(jnp.float32)
    return y.astype(x.dtype)


def swiglu(x, w_gu, w_down):
    g, u = jnp.split(x @ w_gu, 2, axis=-1)
    return (jax.nn.silu(g) * u) @ w_down


def rotary(x, pos):
    half = x.shape[-1] // 2
    inv = ROPE_THETA ** (-jnp.arange(half, dtype=jnp.float32) / half)
    ang = pos.astype(jnp.float32)[:, None] * inv[None, :]
    cos, sin = jnp.cos(ang), jnp.sin(ang)
    xf = x.astype(jnp.float32)
    x1, x2 = xf[..., :half], xf[..., half:]
    return jnp.concatenate([x1 * cos - x2 * sin, x2 * cos + x1 * sin], axis=-1).astype(x.dtype)


def moba_attention(q, k, v):
    Bn, H, S, dh = q.shape
    nb = -(-S // MOBA_BLOCK)
    pad = nb * MOBA_BLOCK - S
    kb = jnp.pad(k, ((0, 0), (0, 0), (0, pad), (0, 0))).reshape(Bn, H, nb, MOBA_BLOCK, dh)
    vb = jnp.pad(v, ((0, 0), (0, 0), (0, pad), (0, 0))).reshape(Bn, H, nb, MOBA_BLOCK, dh)
    k_mean = jnp.mean(kb.astype(jnp.float32), axis=3).astype(q.dtype)
    pos = jnp.arange(S)
    q_blk = pos // MOBA_BLOCK
    gate = jnp.einsum('bhsd,bhnd->bhsn', q, k_mean).astype(jnp.float32)
    past = jnp.arange(nb)[None, :] < q_blk[:, None]
    gate = jnp.where(past, gate, NEG)
    k_sel = min(MOBA_TOPK, nb)
    _, g_idx = lax.top_k(gate, k_sel)
    g_valid = g_idx < q_blk[None, None, :, None]
    scale = dh ** -0.5
    bi = jnp.arange(Bn)[:, None, None, None]
    hi = jnp.arange(H)[None, :, None, None]

    def step(c):
        start = c * MOBA_QCHUNK
        qc = lax.dynamic_slice_in_dim(q, start, MOBA_QCHUNK, axis=2)
        idx = lax.dynamic_slice_in_dim(g_idx, start, MOBA_QCHUNK, axis=2)
        val = lax.dynamic_slice_in_dim(g_valid, start, MOBA_QCHUNK, axis=2)
        blk = start // MOBA_BLOCK
        k_own = lax.dynamic_index_in_dim(kb, blk, axis=2, keepdims=False)
        v_own = lax.dynamic_index_in_dim(vb, blk, axis=2, keepdims=False)
        k_g = kb[bi, hi, idx]
        v_g = vb[bi, hi, idx]
        q_pos = start + jnp.arange(MOBA_QCHUNK)
        k_pos = blk * MOBA_BLOCK + jnp.arange(MOBA_BLOCK)
        s_own = jnp.einsum('bhqd,bhkd->bhqk', qc, k_own).astype(jnp.float32) * scale
        s_own = jnp.where(k_pos[None, :] <= q_pos[:, None], s_own, NEG)
        s_g = jnp.einsum('bhqd,bhqnkd->bhqnk', qc, k_g).astype(jnp.float32) * scale
        s_g = jnp.where(val[..., None], s_g, NEG)
        s = jnp.concatenate([s_own, s_g.reshape(Bn, H, MOBA_QCHUNK, k_sel * MOBA_BLOCK)], axis=-1)
        p = jax.nn.softmax(s, axis=-1).astype(v.dtype)
        p_own = p[..., :MOBA_BLOCK]
        p_g = p[..., MOBA_BLOCK:].reshape(Bn, H, MOBA_QCHUNK, k_sel, MOBA_BLOCK)
        return (jnp.einsum('bhqk,bhkd->bhqd', p_own, v_own)
                + jnp.einsum('bhqnk,bhqnkd->bhqd', p_g, v_g))

    outs = lax.map(step, jnp.arange(S // MOBA_QCHUNK))
    return jnp.moveaxis(outs, 0, 2).reshape(Bn, H, S, dh)


def spatial_gating(u, vg, ln_g, ln_b, w_s, b_s):
    Bn, S, _ = u.shape
    vg = layer_norm(vg, ln_g, ln_b)
    nc = S // SGU_CHUNK
    vr = vg.reshape(Bn, nc, SGU_CHUNK, B_GROUPS, B_DG)
    causal = jnp.tril(jnp.ones((SGU_CHUNK, SGU_CHUNK), dtype=bool))
    w = jnp.where(causal, w_s, jnp.zeros_like(w_s))
    mixed = jnp.einsum('gts,bcsgd->bctgd', w, vr) + b_s.T[None, None, :, :, None]
    return u * mixed.reshape(Bn, S, B_WIDTH)


def mixer_ab(x, w_in, sgu_ln_g, sgu_ln_b, sgu_w, sgu_b, w_out):
    Bn, S, _ = x.shape
    proj = x @ w_in
    qa, ka, va, ub, vb = jnp.split(
        proj, [A_WIDTH, 2 * A_WIDTH, 3 * A_WIDTH, 3 * A_WIDTH + B_WIDTH], axis=-1)
    heads = lambda t: t.reshape(Bn, S, A_HEADS, A_DH).transpose(0, 2, 1, 3)
    pos = jnp.arange(S)
    a = moba_attention(rotary(heads(qa), pos), rotary(heads(ka), pos), heads(va))
    a = a.transpose(0, 2, 1, 3).reshape(Bn, S, A_WIDTH)
    b = spatial_gating(jax.nn.gelu(ub), jax.nn.gelu(vb), sgu_ln_g, sgu_ln_b, sgu_w, sgu_b)
    return jnp.concatenate([a, b], axis=-1) @ w_out


def causal_depthwise_conv(x, w, b):
    K, C = w.shape
    y = lax.conv_general_dilated(x, w[:, None, :], window_strides=(1,), padding=[(K - 1, 0)],
                                 dimension_numbers=('NWC', 'WIO', 'NWC'), feature_group_count=C)
    return y + b


def mlstm_chunkwise(q, k, v, i_pre, f_pre):
    Bn, H, S, dk = q.shape
    dv = v.shape[-1]
    L = MLSTM_CHUNK
    nc = S // L
    f32 = jnp.float32
    qc = q.astype(f32).reshape(Bn, H, nc, L, dk)
    kc = (k.astype(f32) * dk ** -0.5).reshape(Bn, H, nc, L, dk)
    vc = v.astype(f32).reshape(Bn, H, nc, L, dv)
    ic = i_pre.reshape(Bn, H, nc, L)
    b = jnp.cumsum(jax.nn.log_sigmoid(f_pre).reshape(Bn, H, nc, L), axis=-1)
    b_end = b[..., -1]
    a = b_end[..., None] - b + ic
    a_max = jnp.max(a, axis=-1)
    w_end = jnp.exp(a - a_max[..., None])
    kv = jnp.einsum('bhcsk,bhcsv,bhcs->bhckv', kc, vc, w_end)
    ks = jnp.einsum('bhcsk,bhcs->bhck', kc, w_end)

    def scan_fn(carry, inp):
        C, n, m = carry
        kv_c, ks_c, be_c, am_c = inp
        m_new = jnp.maximum(be_c + m, am_c)
        decay = jnp.exp(be_c + m - m_new)
        inject = jnp.exp(am_c - m_new)
        C_new = decay[..., None, None] * C + inject[..., None, None] * kv_c
        n_new = decay[..., None] * n + inject[..., None] * ks_c
        return (C_new, n_new, m_new), (C, n, m)

    init = (jnp.zeros((Bn, H, dk, dv), f32), jnp.zeros((Bn, H, dk), f32), jnp.zeros((Bn, H), f32))
    xs = (jnp.moveaxis(kv, 2, 0), jnp.moveaxis(ks, 2, 0), jnp.moveaxis(b_end, 2, 0), jnp.moveaxis(a_max, 2, 0))
    _, (C_in, n_in, m_in) = lax.scan(scan_fn, init, xs)
    C_in = jnp.moveaxis(C_in, 0, 2)
    n_in = jnp.moveaxis(n_in, 0, 2)
    m_in = jnp.moveaxis(m_in, 0, 2)

    causal = jnp.tril(jnp.ones((L, L), dtype=bool))
    D = b[..., :, None] - b[..., None, :] + ic[..., None, :]
    D = jnp.where(causal, D, -jnp.inf)
    g = b + m_in[..., None]
    m_t = jnp.maximum(g, jnp.max(D, axis=-1))
    P = jnp.exp(D - m_t[..., None])
    sqk = jnp.einsum('bhctk,bhcsk->bhcts', qc, kc) * P
    inter = jnp.exp(g - m_t)
    num = (jnp.einsum('bhcts,bhcsv->bhctv', sqk, vc)
           + inter[..., None] * jnp.einsum('bhctk,bhckv->bhctv', qc, C_in))
    den = jnp.sum(sqk, axis=-1) + inter * jnp.einsum('bhctk,bhck->bhct', qc, n_in)
    h = num / jnp.maximum(jnp.abs(den), jnp.exp(-m_t))[..., None]
    return h.reshape(Bn, H, S, dv)


def mixer_c(x, w_in, conv_w, conv_b, b_i, b_f, head_g, w_out):
    Bn, S, _ = x.shape
    proj = x @ w_in
    qk, v, o_pre, if_pre = jnp.split(
        proj, [2 * C_QK_WIDTH, 2 * C_QK_WIDTH + C_WIDTH, 2 * C_QK_WIDTH + 2 * C_WIDTH], axis=-1)
    qk = jax.nn.silu(causal_depthwise_conv(qk, conv_w, conv_b))
    q, k = jnp.split(qk, 2, axis=-1)
    q = q.reshape(Bn, S, C_HEADS, C_DQK).transpose(0, 2, 1, 3)
    k = k.reshape(Bn, S, C_HEADS, C_DQK).transpose(0, 2, 1, 3)
    v = v.reshape(Bn, S, C_HEADS, C_DV).transpose(0, 2, 1, 3)
    if_f = if_pre.astype(jnp.float32)
    i_pre = (if_f[..., :C_HEADS] + b_i.astype(jnp.float32)).transpose(0, 2, 1)
    f_pre = (if_f[..., C_HEADS:] + b_f.astype(jnp.float32)).transpose(0, 2, 1)
    h = mlstm_chunkwise(q, k, v, i_pre, f_pre)
    mu = jnp.mean(h, axis=-1, keepdims=True)
    var = jnp.mean(jnp.square(h - mu), axis=-1, keepdims=True)
    h = (h - mu) * lax.rsqrt(var + LN_EPS) * head_g.astype(jnp.float32).reshape(C_HEADS, 1, C_DV)
    h = h.transpose(0, 2, 1, 3).reshape(Bn, S, C_WIDTH).astype(x.dtype)
    return (jax.nn.sigmoid(o_pre) * h) @ w_out


def setup_inputs(seed: int = 0) -> dict:
    key = jax.random.key(seed)
    ks = jax.random.split(key, 24)
    n_even = (DEPTH + 1) // 2
    n_odd = DEPTH // 2
    nrm = lambda kk, shape, s: s * jax.random.normal(kk, shape, jnp.float32)
    x = nrm(ks[0], (BATCH, SEQ, D_MODEL), 1.0)
    ln_g = 1.0 + nrm(ks[1], (DEPTH, 3, D_MODEL), 0.02)
    ln_b = nrm(ks[2], (DEPTH, 3, D_MODEL), 0.02)
    ffn_w_gu = nrm(ks[3], (DEPTH, 2, D_MODEL, 2 * D_FF), D_MODEL ** -0.5)
    ffn_w_down = nrm(ks[4], (DEPTH, 2, D_FF, D_MODEL), BETA * D_FF ** -0.5)
    ab_cols = jnp.concatenate([jnp.ones((2 * A_WIDTH,), jnp.float32),
                               jnp.full((A_WIDTH,), BETA, jnp.float32),
                               jnp.ones((2 * B_WIDTH,), jnp.float32)])
    ab_w_in = nrm(ks[5], (n_even, D_MODEL, AB_IN), D_MODEL ** -0.5) * ab_cols
    sgu_ln_g = 1.0 + nrm(ks[6], (n_even, B_WIDTH), 0.02)
    sgu_ln_b = nrm(ks[7], (n_even, B_WIDTH), 0.02)
    sgu_w = nrm(ks[8], (n_even, B_GROUPS, SGU_CHUNK, SGU_CHUNK), SGU_CHUNK ** -0.5)
    sgu_b = 1.0 + nrm(ks[9], (n_even, B_GROUPS, SGU_CHUNK), 0.02)
    ab_w_out = nrm(ks[10], (n_even, AB_MIX, D_MODEL), BETA * AB_MIX ** -0.5)
    c_cols = jnp.concatenate([jnp.ones((2 * C_QK_WIDTH,), jnp.float32),
                              jnp.full((C_WIDTH,), BETA, jnp.float32),
                              jnp.ones((C_WIDTH + 2 * C_HEADS,), jnp.float32)])
    c_w_in = nrm(ks[11], (n_odd, D_MODEL, C_IN), D_MODEL ** -0.5) * c_cols
    c_conv_w = nrm(ks[12], (n_odd, CONV_W, 2 * C_QK_WIDTH), CONV_W ** -0.5)
    c_conv_b = nrm(ks[13], (n_odd, 2 * C_QK_WIDTH), 0.02)
    c_b_i = nrm(ks[14], (n_odd, C_HEADS), 0.1)
    c_b_f = jnp.linspace(3.0, 6.0, C_HEADS, dtype=jnp.float32)[None, :] + nrm(ks[15], (n_odd, C_HEADS), 0.1)
    c_head_g = 1.0 + nrm(ks[16], (n_odd, C_WIDTH), 0.02)
    c_w_out = nrm(ks[17], (n_odd, C_WIDTH, D_MODEL), BETA * C_WIDTH ** -0.5)
    return {"x": x, "ln_g": ln_g, "ln_b": ln_b, "ffn_w_gu": ffn_w_gu, "ffn_w_down": ffn_w_down,
            "ab_w_in": ab_w_in, "sgu_ln_g": sgu_ln_g, "sgu_ln_b": sgu_ln_b, "sgu_w": sgu_w,
            "sgu_b": sgu_b, "ab_w_out": ab_w_out, "c_w_in": c_w_in, "c_conv_w": c_conv_w,
            "c_conv_b": c_conv_b, "c_b_i": c_b_i, "c_b_f": c_b_f, "c_head_g": c_head_g,
            "c_w_out": c_w_out}


def reference(x, ln_g, ln_b, ffn_w_gu, ffn_w_down, ab_w_in, sgu_ln_g, sgu_ln_b, sgu_w, sgu_b,
              ab_w_out, c_w_in, c_conv_w, c_conv_b, c_b_i, c_b_f, c_head_g, c_w_out):
    for l in range(DEPTH):
        x = layer_norm(ALPHA * x + 0.5 * swiglu(x, ffn_w_gu[l, 0], ffn_w_down[l, 0]), ln_g[l, 0], ln_b[l, 0])
        j = l // 2
        if l % 2 == 0:
            y = mixer_ab(x, ab_w_in[j], sgu_ln_g[j], sgu_ln_b[j], sgu_w[j], sgu_b[j], ab_w_out[j])
        else:
            y = mixer_c(x, c_w_in[j], c_conv_w[j], c_conv_b[j], c_b_i[j], c_b_f[j], c_head_g[j], c_w_out[j])
        x = layer_norm(ALPHA * x + y, ln_g[l, 1], ln_b[l, 1])
        x = layer_norm(ALPHA * x + 0.5 * swiglu(x, ffn_w_gu[l, 1], ffn_w_down[l, 1]), ln_g[l, 2], ln_b[l, 2])
    return x
```

```python
import os
import numpy as np
from contextlib import ExitStack
import concourse.bass as bass
import concourse.mybir as mybir
from concourse.bass_utils import run_bass_kernel_spmd

F32 = mybir.dt.float32
BF16 = mybir.dt.bfloat16
AF = mybir.ActivationFunctionType
ALU = mybir.AluOpType
AX = mybir.AxisListType

D = 1024
DFF = 2816
NFC = DFF // 128
DEPTH = 2
ALPHA = (2 * DEPTH) ** 0.25
LN_EPS = 1e-5
NEGB = -30000.0
TG = 1024
NT = TG // 128
AB_IN = 2560
C_IN = 3080
N_CORES = 8


class Op:
    __slots__ = ("eng", "fn", "deps", "sem", "inc", "needed", "val", "is_dma", "alias")


class Tk:
    __slots__ = ("w", "rs")

    def __init__(self):
        self.w = None
        self.rs = []


class Sem:
    def __init__(self, nc, name):
        self.h = nc.alloc_semaphore(name=name)
        self.count = 0
        self.last = None


class Buf:
    def __init__(self, t, tk=None):
        self.t = t
        self.tk = tk if tk is not None else Tk()


ENGS = ("pe", "act", "dve", "pool", "sp")


class Prog:
    def __init__(self, nc):
        self.nc = nc
        self.q = {e: [] for e in ENGS}
        self.trace = []
        self.esem = {e: Sem(nc, "es_" + e) for e in ("pe", "act", "dve", "pool")}

    def op(self, eng, fn, r=(), w=(), dsem=None, grp=False):
        o = Op()
        o.eng = eng
        o.fn = fn
        o.is_dma = dsem is not None
        raw = []
        oth = []
        for b in r:
            t = b.tk if isinstance(b, Buf) else b
            if t.w is not None:
                raw.append(t.w)
        for b in w:
            t = b.tk if isinstance(b, Buf) else b
            if t.w is not None:
                oth.append(t.w)
            oth.extend(t.rs)
        if o.is_dma:
            o.sem = dsem
            o.inc = 16
            o.needed = True
            if dsem.last is not None:
                if grp:
                    dsem.last.alias = o
                else:
                    oth.append(dsem.last)
            dsem.last = o
        else:
            o.sem = self.esem[eng]
            o.inc = 1
            o.needed = False
        deps = []
        seen = set()
        for d in raw:
            if id(d) in seen:
                continue
            if (not d.is_dma) and d.eng == eng and eng == "pe":
                continue
            seen.add(id(d))
            deps.append(d)
        for d in oth:
            if id(d) in seen:
                continue
            if (not d.is_dma) and d.eng == eng and (not o.is_dma):
                continue
            seen.add(id(d))
            deps.append(d)
        if o.is_dma and grp:
            deps = [d for d in deps if not (d.is_dma and d.sem is dsem)]
        for d in deps:
            d.needed = True
        o.deps = deps
        o.val = None
        o.alias = None
        self.q[eng].append(o)
        self.trace.append(o)
        for b in r:
            t = b.tk if isinstance(b, Buf) else b
            if not o.is_dma:
                t.rs = [x for x in t.rs if x.is_dma or x.eng != eng]
            t.rs.append(o)
        for b in w:
            t = b.tk if isinstance(b, Buf) else b
            t.w = o
            t.rs = []
        return o

    def finalize(self):
        for o in self.trace:
            if o.needed:
                o.sem.count += o.inc
                o.val = o.sem.count

    def emit(self, eng, e):
        waited = {}
        for o in self.q[eng]:
            need = {}
            for d in o.deps:
                while d.alias is not None:
                    d = d.alias
                s = d.sem
                if waited.get(s, 0) >= d.val:
                    continue
                if need.get(s, 0) < d.val:
                    need[s] = d.val
            for s, v in need.items():
                e.wait_ge(s.h, v)
                waited[s] = v
            if o.fn is None:
                continue
            ins = o.fn(e)
            if o.needed:
                ins.then_inc(o.sem.h, o.inc)

    def run_block(self):
        nc = self.nc
        self.finalize()
        with nc.Block() as block:
            @block.tensor
            def _(e):
                self.emit("pe", e)

            @block.scalar
            def _(e):
                self.emit("act", e)

            @block.vector
            def _(e):
                self.emit("dve", e)

            @block.gpsimd
            def _(e):
                self.emit("pool", e)

            @block.sync
            def _(e):
                self.emit("sp", e)


def _pieces_k1024(W, ncol=512):
    K, N = W.shape
    assert K == 1024 and N % ncol == 0
    return np.ascontiguousarray(W.reshape(8, 128, N // ncol, ncol).transpose(2, 1, 0, 3))


def _lay_gu(W):
    g = W[:, :DFF].reshape(8, 128, 11, 256)
    u = W[:, DFF:].reshape(8, 128, 11, 256)
    p = np.concatenate([g, u], axis=3)
    return np.ascontiguousarray(p.transpose(2, 1, 0, 3))


def _lay_down(W):
    return np.ascontiguousarray(W.reshape(NFC, 128, 4, 256).transpose(2, 1, 0, 3))


class Builder:
    def __init__(self, S, dbg=False):
        self.S = S
        self.NG = S // TG
        self.dbg = dbg
        self.nc = bass.Bass("TRN2", target_bir_lowering=False)
        self.P = Prog(self.nc)
        self.es = None
        self.pfx = "s_"
        self.tks = []
        self.semcache = {}
        self.dsems = []
        self.dt = {}
        self.dtk = {}

    def sb(self, name, shape, dt):
        return self.es.enter_context(self.nc.sbuf_tensor(self.pfx + name, list(shape), dt))

    def ps(self, name, shape, dt):
        return self.es.enter_context(self.nc.psum_tensor(self.pfx + name, list(shape), dt))

    def tk(self):
        t = Tk()
        self.tks.append(t)
        return t

    def buf(self, t):
        return Buf(t, self.tk())

    def sbuf(self, name, shape, dt):
        return self.buf(self.sb(name, shape, dt))

    def pbuf(self, name, shape, dt):
        return self.buf(self.ps(name, shape, dt))

    def dsem(self, name):
        if name in self.semcache:
            return self.semcache[name]
        s = Sem(self.nc, name)
        self.semcache[name] = s
        self.dsems.append(s)
        return s

    def dram_in(self, name, shape, dt=F32):
        t = self.nc.dram_tensor(name, list(shape), dt, kind="ExternalInput")
        self.dt[name] = t
        return t

    def dram_out(self, name, shape, dt=F32):
        t = self.nc.dram_tensor(name, list(shape), dt, kind="ExternalOutput")
        self.dt[name] = t
        return t

    def dram_tmp(self, name, shape, dt):
        kind = "ExternalOutput" if self.dbg else "Internal"
        if name in getattr(self, "ext_in", ()):
            kind = "ExternalInput"
        t = self.nc.dram_tensor(name, list(shape), dt, kind=kind)
        self.dt[name] = t
        return t

    def dk(self, name, idx=0):
        k = (name, idx)
        if k not in self.dtk:
            self.dtk[k] = self.tk()
        return self.dtk[k]

    def dma(self, eng, out, in_, r, w, sem, grp=False):
        fn = lambda e, o=out, i=in_: e.dma_start(out=o, in_=i)
        return self.P.op(eng, fn, r=r, w=w, dsem=sem, grp=grp)

    def barrier(self):
        P = self.P
        lasts = []
        for e in ("pe", "act", "dve", "pool"):
            for o in reversed(P.q[e]):
                if o.fn is not None and not o.is_dma:
                    lasts.append(o)
                    break
        for s in self.dsems:
            if s.last is not None:
                lasts.append(s.last)
        for o in lasts:
            o.needed = True
        for e in ENGS:
            o = Op()
            o.eng = e
            o.fn = None
            o.is_dma = False
            o.sem = P.esem.get(e)
            o.inc = 1
            o.needed = False
            o.val = None
            o.alias = None
            o.deps = [d for d in lasts if not ((not d.is_dma) and d.eng == e)]
            P.q[e].append(o)
            P.trace.append(o)
        for t in self.tks:
            t.w = None
            t.rs = []
        for s in self.dsems:
            s.last = None

    def end_phase(self):
        self.barrier()
        self.P.run_block()
        self.P.q = {e: [] for e in ENGS}
        self.P.trace = []

    def declare(self):
        S = self.S
        di = self.dram_in
        di("x", [S, D])
        di("wgu", [4, 11, 128, 4096])
        di("wdn", [4, 4, 128, 5632])
        di("wab_in", [5, 128, 4096])
        di("wab_out", [2, 128, 4096])
        di("wc_in", [6, 128, 4096])
        di("wc_if", [128, 64])
        di("wc_out", [2, 128, 4096])
        di("ln_g", [6, D])
        di("ln_b", [6, D])
        di("sgu_lng", [512])
        di("sgu_lnb", [512])
        di("sgu_wT", [128, 8 * 128])
        di("sgu_bT", [128, 8])
        di("conv_w", [128, 32])
        di("conv_b", [128, 8])
        di("bif", [8])
        di("head_g", [D])
        di("ident", [128, 128])
        di("triu", [128, 128])
        di("rope", [S, 64])
        di("onehot", [32, S])
        di("dmask", [4, 128, 512])
        self.dram_out("y", [S, D])
        tmp = self.dram_tmp
        tmp("wgu_b", [4, 11, 128, 4096], BF16)
        tmp("wdn_b", [4, 4, 128, 5632], BF16)
        tmp("wab_in_b", [5, 128, 4096], BF16)
        tmp("wab_out_b", [2, 128, 4096], BF16)
        tmp("wc_in_b", [6, 128, 4096], BF16)
        tmp("wc_out_b", [2, 128, 4096], BF16)
        tmp("xr", [S, D], F32)
        tmp("qT_d", [512, S], BF16)
        tmp("kT_d", [512, S], BF16)
        tmp("v_d", [S, 512], BF16)
        tmp("yT_ab", [D, S], BF16)
        tmp("qkT_c", [D, S], BF16)
        tmp("v_c", [S, D], BF16)
        tmp("o_c", [S, D], BF16)
        tmp("gates", [S, 16], F32)
        tmp("yT_c", [D, S], BF16)
        if self.dbg:
            tmp("dbg_h", [S, D], F32)

    def convert_weights(self):
        def conv(src, dst, idxs, tkname):
            s = self.dsem("cv_" + tkname)
            first = True
            for ix in idxs:
                a = self.dt[src]
                b = self.dt[dst]
                for i in ix:
                    a = a[i]
                    b = b[i]
                self.dma("pool", b, a, r=[], w=[self.dk(tkname)], sem=s, grp=not first)
                first = False
        order = [("gu", 0), ("dn", 0), ("ab_in",), ("ab_out",), ("gu", 1), ("dn", 1),
                 ("gu", 2), ("dn", 2), ("c_in",), ("c_out",), ("gu", 3), ("dn", 3)]
        for it in order:
            if it[0] == "gu":
                f = it[1]
                conv("wgu", "wgu_b", [(f, j) for j in range(11)], "wgu%d" % f)
            elif it[0] == "dn":
                f = it[1]
                conv("wdn", "wdn_b", [(f, j) for j in range(4)], "wdn%d" % f)
            elif it[0] == "ab_in":
                conv("wab_in", "wab_in_b", [(j,) for j in range(5)], "wab_in")
            elif it[0] == "ab_out":
                conv("wab_out", "wab_out_b", [(j,) for j in range(2)], "wab_out")
            elif it[0] == "c_in":
                conv("wc_in", "wc_in_b", [(j,) for j in range(6)], "wc_in")
            elif it[0] == "c_out":
                conv("wc_out", "wc_out_b", [(j,) for j in range(2)], "wc_out")

    def alloc_R(self, kind):
        B = self
        B.res = B.sb("res", [128, NT, D], F32)
        B.res_tk = [B.tk() for _ in range(NT)]
        B.xb = [B.sbuf("xb%d" % i, [128, D], BF16) for i in range(2)]
        B.xT = B.sb("xT", [128, 8, TG], BF16)
        B.xT_tk = [B.tk() for _ in range(NT)]
        B.hT = B.sb("hT", [128, NFC, TG], BF16)
        B.hT_tk = [B.tk() for _ in range(2)]
        B.sg = [B.sbuf("sg%d" % i, [128, 512], F32) for i in range(2)]
        B.wp = [B.sbuf("wp%d" % i, [128, 5632], BF16) for i in range(3)]
        B.wp_sem = [B.dsem("wp%d" % i) for i in range(3)]
        B.wp_i = 0
        B.nlnp = 1 if kind == 0 else 2
        B.lnp = [B.sbuf("lnp%d" % i, [128, 2, D], F32) for i in range(B.nlnp)]
        B.lnp_sem = [B.dsem("lnp%d" % i) for i in range(B.nlnp)]
        B.lnp_i = 0
        B.st = [B.sbuf("st%d" % i, [128, 16], F32) for i in range(4)]
        B.st_i = 0
        B.identf = B.sbuf("identf", [128, 128], F32)
        B.ident = B.sbuf("ident", [128, 128], BF16)
        B.pa = [B.pbuf("pa%d" % i, [128, 512], F32) for i in range(4)]
        B.pa_i = 0
        B.pd = [B.pbuf("pd%d" % i, [128, 512], F32) for i in range(2)]
        B.pd_i = 0
        B.pt = [B.pbuf("pt%d" % i, [128, 1024], BF16) for i in range(2)]
        B.pt_i = 0
        B.ld_sem = [B.dsem("ld%d" % i) for i in range(4)]
        B.ld_i = 0
        B.st_sem = [B.dsem("stq%d" % i) for i in range(4)]
        B.sts_i = 0
        s = B.dsem("cst")
        B.dma("sp", B.identf.t[:], B.dt["ident"].ap(), r=[], w=[B.identf], sem=s)
        B.P.op("dve", lambda e: e.tensor_copy(out=B.ident.t[:], in_=B.identf.t[:]),
               r=[B.identf], w=[B.ident])

    def next_ld(self):
        s = self.ld_sem[self.ld_i % 4]
        self.ld_i += 1
        return s

    def next_st(self):
        s = self.st_sem[self.sts_i % 4]
        self.sts_i += 1
        return s

    def next_pa(self):
        b = self.pa[self.pa_i % 4]
        self.pa_i += 1
        return b

    def load_wp(self, src_ap, nelem, rtk):
        i = self.wp_i % 3
        self.wp_i += 1
        b = self.wp[i]
        self.dma("sp", b.t[:, 0:nelem], src_ap, r=[rtk], w=[b], sem=self.wp_sem[i])
        return b

    def make_xT(self, t, scale_res=True):
        B = self
        P = B.P
        rt = Buf(None, B.res_tk[t])
        xb = B.xb[t % 2]
        P.op("act", lambda e: e.copy(out=xb.t[:], in_=B.res[:, t, :]), r=[rt], w=[xb])
        pt = B.pt[B.pt_i % 2]
        B.pt_i += 1
        for kc in range(8):
            P.op("pe", lambda e, kc=kc: e.transpose(out=pt.t[:, kc * 128:(kc + 1) * 128],
                                                    in_=xb.t[:, kc * 128:(kc + 1) * 128],
                                                    identity=B.ident.t[:]),
                 r=[xb, B.ident], w=[pt])
        xt = Buf(None, B.xT_tk[t])
        P.op("act", lambda e: e.copy(out=B.xT[:, :, t * 128:(t + 1) * 128],
                                     in_=pt.t[:].rearrange("p (k n) -> p k n", k=8)),
             r=[pt], w=[xt])
        if scale_res:
            P.op("act", lambda e: e.mul(out=B.res[:, t, :], in_=B.res[:, t, :], mul=ALPHA),
                 r=[rt], w=[rt])

    def ffn(self, f):
        B = self
        P = B.P
        gu_tk = B.dk("wgu%d" % f)
        dn_tk = B.dk("wdn%d" % f)
        xts = [Buf(None, k) for k in B.xT_tk]
        for j in range(11):
            wp = B.load_wp(B.dt["wgu_b"][f][j], 4096, gu_tk)
            for hf in range(2):
                hb = Buf(None, B.hT_tk[hf])
                for fc in range(2):
                    pg = B.next_pa()
                    pu = B.next_pa()
                    for (pp, off) in ((pg, 0), (pu, 256)):
                        for kc in range(8):
                            P.op("pe", lambda e, pp=pp, off=off, kc=kc, hf=hf, fc=fc, wp=wp: e.matmul(
                                pp.t[:], wp.t[:, kc * 512 + off + fc * 128: kc * 512 + off + fc * 128 + 128],
                                B.xT[:, kc, hf * 512:(hf + 1) * 512], start=(kc == 0), stop=(kc == 7)),
                                r=[wp] + xts[hf * 4:(hf + 1) * 4], w=[pp])
                    sg = B.sg[(hf * 2 + fc) % 2]
                    P.op("act", lambda e, sg=sg, pg=pg: e.activation(out=sg.t[:], in_=pg.t[:], func=AF.Silu),
                         r=[pg], w=[sg])
                    c = 2 * j + fc
                    P.op("dve", lambda e, sg=sg, pu=pu, c=c, hf=hf: e.tensor_tensor(
                        out=B.hT[:, c, hf * 512:(hf + 1) * 512], in0=sg.t[:], in1=pu.t[:], op=ALU.mult),
                        r=[sg, pu], w=[hb])
        hbs = [Buf(None, k) for k in B.hT_tk]
        for cb in range(4):
            wp = B.load_wp(B.dt["wdn_b"][f][cb], 5632, dn_tk)
            for t in range(NT):
                pd = B.pd[B.pd_i % 2]
                B.pd_i += 1
                for c in range(NFC):
                    P.op("pe", lambda e, pd=pd, c=c, t=t, wp=wp: e.matmul(
                        pd.t[:, 0:256], B.hT[:, c, t * 128:(t + 1) * 128], wp.t[:, c * 256:(c + 1) * 256],
                        start=(c == 0), stop=(c == NFC - 1)),
                        r=[wp, hbs[t // 4]], w=[pd])
                rt = Buf(None, B.res_tk[t])
                P.op("dve", lambda e, pd=pd, t=t, cb=cb: e.scalar_tensor_tensor(
                    out=B.res[:, t, cb * 256:(cb + 1) * 256], in0=pd.t[:, 0:256], scalar=0.5,
                    in1=B.res[:, t, cb * 256:(cb + 1) * 256], op0=ALU.mult, op1=ALU.add),
                    r=[pd, rt], w=[rt])

    def load_lnp(self, idx):
        B = self
        i = B.lnp_i % B.nlnp
        B.lnp_i += 1
        b = B.lnp[i]
        B.dma("sp", b.t[:, 0, :], B.dt["ln_g"].ap()[idx].partition_broadcast(128), r=[], w=[b],
              sem=B.lnp_sem[i])
        B.dma("sp", b.t[:, 1, :], B.dt["ln_b"].ap()[idx].partition_broadcast(128), r=[], w=[b],
              sem=B.lnp_sem[i], grp=True)
        return b

    def ln_stats(self, src_aps, rtk):
        B = self
        P = B.P
        st = B.st[B.st_i % 4]
        B.st_i += 1
        n = len(src_aps)
        for i, a in enumerate(src_aps):
            P.op("dve", lambda e, i=i, a=a: e.bn_stats(out=st.t[:, 6 * i:6 * i + 6], in_=a), r=[rtk], w=[st])
        P.op("dve", lambda e: e.bn_aggr(out=st.t[:, 12:14], in_=st.t[:, 0:6 * n]), r=[st], w=[st])
        P.op("act", lambda e: e.activation(out=st.t[:, 14:15], in_=st.t[:, 13:14], func=AF.Sqrt,
                                           bias=B.eps.t[:, 0:1], scale=1.0), r=[st, B.eps], w=[st])
        P.op("dve", lambda e: e.reciprocal(out=st.t[:, 15:16], in_=st.t[:, 14:15]), r=[st], w=[st])
        return st, st.t[:, 12:13], st.t[:, 15:16]

    def layer_norm(self, t, lnp):
        B = self
        P = B.P
        rt = Buf(None, B.res_tk[t])
        st, mean, rstd = B.ln_stats([B.res[:, t, 0:512], B.res[:, t, 512:1024]], rt)
        P.op("dve", lambda e: e.tensor_scalar(out=B.res[:, t, :], in0=B.res[:, t, :], scalar1=mean, scalar2=rstd,
                                              op0=ALU.subtract, op1=ALU.mult), r=[st, rt], w=[rt])
        P.op("dve", lambda e: e.tensor_tensor(out=B.res[:, t, :], in0=B.res[:, t, :], in1=lnp.t[:, 0, :],
                                              op=ALU.mult), r=[rt, lnp], w=[rt])
        P.op("dve", lambda e: e.tensor_tensor(out=B.res[:, t, :], in0=B.res[:, t, :], in1=lnp.t[:, 1, :],
                                              op=ALU.add), r=[rt, lnp], w=[rt])

    def alloc_consts_R(self):
        B = self
        B.eps = B.sbuf("eps", [128, 1], F32)
        B.P.op("dve", lambda e: e.memset(B.eps.t[:], LN_EPS), r=[], w=[B.eps])

    def store_res(self, t, dname, tok0):
        B = self
        rt = Buf(None, B.res_tk[t])
        B.dma("pool", B.dt[dname].ap()[tok0:tok0 + 128, :], B.res[:, t, :], r=[rt],
              w=[B.dk(dname, tok0 // TG)], sem=B.next_st())

    def load_res(self, t, dname, tok0):
        B = self
        rt = Buf(None, B.res_tk[t])
        B.dma("sp", B.res[:, t, :], B.dt[dname].ap()[tok0:tok0 + 128, :], r=[B.dk(dname, tok0 // TG)],
              w=[rt], sem=B.next_ld())

    def alloc_AB(self):
        B = self
        B.rope_sb = B.sbuf("rope_sb", [128, NT, 64], F32)
        B.rope_sem = B.dsem("rope")
        B.qo = [B.sbuf("qo%d" % i, [128, 512], BF16) for i in range(3)]
        B.qo_i = 0
        B.rt = [B.sbuf("rt%d" % i, [128, 4, 256], F32) for i in range(2)]
        B.rt_i = 0
        B.ug = B.sb("ug", [128, NT, 512], BF16)
        B.ug_tk = [B.tk() for _ in range(NT)]
        B.vg = [B.sbuf("vg%d" % i, [128, 512], F32) for i in range(2)]
        B.vn = [B.sbuf("vn%d" % i, [128, 512], BF16) for i in range(2)]
        B.bo = [B.sbuf("bo%d" % i, [128, 512], BF16) for i in range(2)]
        B.qkTs = [B.sbuf("qkTs%d" % i, [128, 4, TG], BF16) for i in range(2)]
        B.bT = B.qkTs[0]
        B.sgl = B.sbuf("sgl", [128, 2, 512], F32)
        B.wsf = Buf(B.rt[0].t[:].rearrange("p k n -> p (k n)"), B.rt[0].tk)
        B.triu = B.sbuf("triu", [128, 128], F32)
        B.wsT = B.sbuf("wsT", [128, 8 * 128], BF16)
        B.bsT = B.sbuf("bsT", [128, 8], F32)
        s = B.dsem("cstab")
        B.dma("sp", B.sgl.t[:, 0, :], B.dt["sgu_lng"].ap().partition_broadcast(128), r=[], w=[B.sgl], sem=s)
        B.dma("sp", B.sgl.t[:, 1, :], B.dt["sgu_lnb"].ap().partition_broadcast(128), r=[], w=[B.sgl], sem=s, grp=True)
        B.dma("sp", B.wsf.t, B.dt["sgu_wT"].ap(), r=[], w=[B.wsf], sem=s, grp=True)
        B.dma("sp", B.triu.t[:], B.dt["triu"].ap(), r=[], w=[B.triu], sem=s, grp=True)
        B.dma("sp", B.bsT.t[:], B.dt["sgu_bT"].ap(), r=[], w=[B.bsT], sem=s, grp=True)
        B.P.op("dve", lambda e: e.tensor_tensor(
            out=B.wsT.t[:].rearrange("p (g t) -> p g t", g=8),
            in0=B.wsf.t.rearrange("p (g t) -> p g t", g=8),
            in1=B.triu.t[:].unsqueeze(1).to_broadcast([128, 8, 128]), op=ALU.mult),
            r=[B.wsf, B.triu], w=[B.wsT])

    def inproj_AB(self, g):
        B = self
        P = B.P
        tok_g = g * TG
        wtk = B.dk("wab_in")
        B.dma("sp", B.rope_sb.t[:], B.dt["rope"].ap()[tok_g:tok_g + TG, :].rearrange("(t p) c -> p t c", p=128),
              r=[], w=[B.rope_sb], sem=B.rope_sem)
        for nb in (0, 1, 2, 3, 4):
            wp = B.load_wp(B.dt["wab_in_b"][nb], 4096, wtk)
            for t in range(NT):
                tok0 = tok_g + t * 128
                xt = Buf(None, B.xT_tk[t])
                pq = B.next_pa()
                for kc in range(8):
                    P.op("pe", lambda e, pq=pq, kc=kc, t=t, wp=wp: e.matmul(
                        pq.t[:], B.xT[:, kc, t * 128:(t + 1) * 128], wp.t[:, kc * 512:(kc + 1) * 512],
                        start=(kc == 0), stop=(kc == 7)), r=[wp, xt], w=[pq])
                if nb in (0, 1):
                    qo = B.qo[B.qo_i % 3]
                    B.qo_i += 1
                    rt = B.rt[B.rt_i % 2]
                    B.rt_i += 1
                    psv = pq.t[:].rearrange("p (h two d) -> p h two d", h=8, two=2)
                    qov = qo.t[:].rearrange("p (h two d) -> p h two d", h=8, two=2)
                    cosb = B.rope_sb.t[:, t, 0:32].unsqueeze(1).to_broadcast([128, 8, 32])
                    sinb = B.rope_sb.t[:, t, 32:64].unsqueeze(1).to_broadcast([128, 8, 32])
                    rv = rt.t[:].rearrange("p k (h d) -> p k h d", h=8)
                    x1 = psv[:, :, 0, :]
                    x2 = psv[:, :, 1, :]
                    for (k, xa, cs) in ((0, x1, cosb), (1, x2, sinb), (2, x2, cosb), (3, x1, sinb)):
                        P.op("dve", lambda e, k=k, xa=xa, cs=cs, rv=rv: e.tensor_tensor(
                            out=rv[:, k], in0=xa, in1=cs, op=ALU.mult), r=[pq, B.rope_sb], w=[rt])
                    P.op("dve", lambda e, rv=rv, qov=qov: e.tensor_tensor(
                        out=qov[:, :, 0, :], in0=rv[:, 0], in1=rv[:, 1], op=ALU.subtract), r=[rt], w=[qo])
                    P.op("dve", lambda e, rv=rv, qov=qov: e.tensor_tensor(
                        out=qov[:, :, 1, :], in0=rv[:, 2], in1=rv[:, 3], op=ALU.add), r=[rt], w=[qo])
                    pt = B.pt[B.pt_i % 2]
                    B.pt_i += 1
                    for c in range(4):
                        P.op("pe", lambda e, pt=pt, c=c, qo=qo: e.transpose(
                            out=pt.t[:, c * 128:(c + 1) * 128], in_=qo.t[:, c * 128:(c + 1) * 128],
                            identity=B.ident.t[:]), r=[qo, B.ident], w=[pt])
                    qkT = B.qkTs[nb]
                    P.op("act", lambda e, pt=pt, t=t, qkT=qkT: e.copy(
                        out=qkT.t[:, :, t * 128:(t + 1) * 128],
                        in_=pt.t[:, 0:512].rearrange("p (k n) -> p k n", k=4)), r=[pt], w=[qkT])
                    if t == NT - 1:
                        dn = "qT_d" if nb == 0 else "kT_d"
                        B.dma("pool", B.dt[dn].ap()[:, tok_g:tok_g + TG].rearrange("(c p) n -> p c n", p=128),
                              qkT.t[:], r=[qkT], w=[B.dk(dn, g)], sem=B.next_st())
                elif nb == 2:
                    qo = B.qo[B.qo_i % 3]
                    B.qo_i += 1
                    P.op("act", lambda e, qo=qo, pq=pq: e.copy(out=qo.t[:], in_=pq.t[:]), r=[pq], w=[qo])
                    B.dma("pool", B.dt["v_d"].ap()[tok0:tok0 + 128, :], qo.t[:], r=[qo], w=[B.dk("v_d", g)],
                          sem=B.next_st())
                elif nb == 3:
                    ut = Buf(None, B.ug_tk[t])
                    P.op("act", lambda e, t=t, pq=pq: e.activation(out=B.ug[:, t, :], in_=pq.t[:],
                                                                   func=AF.Gelu_apprx_tanh), r=[pq], w=[ut])
                else:
                    ut = Buf(None, B.ug_tk[t])
                    vg = B.vg[t % 2]
                    vn = B.vn[t % 2]
                    bo = B.bo[t % 2]
                    P.op("act", lambda e, vg=vg, pq=pq: e.activation(out=vg.t[:], in_=pq.t[:],
                                                                     func=AF.Gelu_apprx_tanh), r=[pq], w=[vg])
                    st, mean, rstd = B.ln_stats([vg.t[:]], vg)
                    P.op("dve", lambda e, vg=vg, mean=mean, rstd=rstd: e.tensor_scalar(
                        out=vg.t[:], in0=vg.t[:], scalar1=mean, scalar2=rstd, op0=ALU.subtract, op1=ALU.mult),
                        r=[st, vg], w=[vg])
                    P.op("dve", lambda e, vg=vg: e.tensor_tensor(out=vg.t[:], in0=vg.t[:], in1=B.sgl.t[:, 0, :],
                                                                 op=ALU.mult), r=[vg, B.sgl], w=[vg])
                    P.op("dve", lambda e, vg=vg, vn=vn: e.tensor_tensor(out=vn.t[:], in0=vg.t[:], in1=B.sgl.t[:, 1, :],
                                                                        op=ALU.add), r=[vg, B.sgl], w=[vn])
                    pm = B.next_pa()
                    for gi in range(8):
                        P.op("pe", lambda e, pm=pm, gi=gi, vn=vn: e.matmul(
                            pm.t[:, gi * 64:(gi + 1) * 64], B.wsT.t[:, gi * 128:(gi + 1) * 128],
                            vn.t[:, gi * 64:(gi + 1) * 64], start=True, stop=True), r=[B.wsT, vn], w=[pm])
                    P.op("dve", lambda e, pm=pm, vg=vg: e.tensor_tensor(
                        out=vg.t[:].rearrange("p (g d) -> p g d", g=8),
                        in0=pm.t[:].rearrange("p (g d) -> p g d", g=8),
                        in1=B.bsT.t[:].unsqueeze(2).to_broadcast([128, 8, 64]), op=ALU.add),
                        r=[pm, B.bsT], w=[vg])
                    P.op("dve", lambda e, vg=vg, bo=bo, t=t: e.tensor_tensor(
                        out=bo.t[:], in0=vg.t[:], in1=B.ug[:, t, :], op=ALU.mult), r=[vg, ut], w=[bo])
                    pt = B.pt[B.pt_i % 2]
                    B.pt_i += 1
                    for c in range(4):
                        P.op("pe", lambda e, pt=pt, c=c, bo=bo: e.transpose(
                            out=pt.t[:, c * 128:(c + 1) * 128], in_=bo.t[:, c * 128:(c + 1) * 128],
                            identity=B.ident.t[:]), r=[bo, B.ident], w=[pt])
                    P.op("act", lambda e, pt=pt, t=t: e.copy(
                        out=B.bT.t[:, :, t * 128:(t + 1) * 128],
                        in_=pt.t[:, 0:512].rearrange("p (k n) -> p k n", k=4)), r=[pt], w=[B.bT])
        B.dma("pool", B.dt["yT_ab"].ap()[512:1024, tok_g:tok_g + TG].rearrange("(c p) n -> p c n", p=128),
              B.bT.t[:], r=[B.bT], w=[B.dk("yT_ab", g)], sem=B.next_st())

    def phase_R0(self):
        B = self
        with ExitStack() as es:
            B.es = es
            B.pfx = "r0_"
            B.alloc_R(0)
            B.alloc_consts_R()
            B.alloc_AB()
            B.convert_weights()
            for g in range(B.NG):
                tok_g = g * TG
                for t in range(NT):
                    B.load_res(t, "x", tok_g + t * 128)
                    B.make_xT(t)
                lnp = B.load_lnp(0)
                B.ffn(0)
                for t in range(NT):
                    B.layer_norm(t, lnp)
                    B.store_res(t, "xr", tok_g + t * 128)
                    B.make_xT(t, scale_res=False)
                B.inproj_AB(g)
            B.end_phase()


def _consts(S):
    c = {}
    c["ident"] = np.eye(128, dtype=np.float32)
    c["triu"] = np.triu(np.ones((128, 128), np.float32))
    half = 32
    inv = (10000.0 ** (-np.arange(half, dtype=np.float32) / half)).astype(np.float32)
    ang = np.arange(S, dtype=np.float32)[:, None] * inv[None, :]
    c["rope"] = np.concatenate([np.cos(ang), np.sin(ang)], axis=1).astype(np.float32)
    oh = np.zeros((32, S), np.float32)
    for n in range(min(32, S // 256)):
        oh[n, n * 256:(n + 1) * 256] = 1.0
    c["onehot"] = oh
    dm = np.zeros((4, 128, 512), np.float32)
    for r in range(4):
        for j in range(4):
            if r // 2 != j // 2:
                dm[r, :, j * 128:(j + 1) * 128] = 1.0
            elif r < j:
                dm[r, :, j * 128:(j + 1) * 128] = 1.0
            elif r == j:
                dm[r, :, j * 128:(j + 1) * 128] = np.triu(np.ones((128, 128), np.float32))
    c["dmask"] = dm
    return c


def prep_weights(inp):
    w = {}
    gu = inp["ffn_w_gu"]
    dn = inp["ffn_w_down"]
    w["wgu"] = np.stack([_lay_gu(gu[l, i]) for l in range(2) for i in range(2)]).reshape(4, 11, 128, 4096)
    w["wdn"] = np.stack([_lay_down(dn[l, i]) for l in range(2) for i in range(2)]).reshape(4, 4, 128, 5632)
    w["wab_in"] = _pieces_k1024(inp["ab_w_in"][0]).reshape(5, 128, 4096)
    w["wab_out"] = _pieces_k1024(inp["ab_w_out"][0]).reshape(2, 128, 4096)
    cin = inp["c_w_in"][0]
    w["wc_in"] = _pieces_k1024(cin[:, :3072]).reshape(6, 128, 4096)
    w["wc_if"] = np.ascontiguousarray(cin[:, 3072:3080].reshape(8, 128, 8).transpose(1, 0, 2)).reshape(128, 64)
    w["wc_out"] = _pieces_k1024(inp["c_w_out"][0]).reshape(2, 128, 4096)
    w["ln_g"] = np.ascontiguousarray(inp["ln_g"].reshape(6, D))
    w["ln_b"] = np.ascontiguousarray(inp["ln_b"].reshape(6, D))
    w["sgu_lng"] = np.ascontiguousarray(inp["sgu_ln_g"][0])
    w["sgu_lnb"] = np.ascontiguousarray(inp["sgu_ln_b"][0])
    w["sgu_wT"] = np.ascontiguousarray(inp["sgu_w"][0].transpose(2, 0, 1)).reshape(128, 1024)
    w["sgu_bT"] = np.ascontiguousarray(inp["sgu_b"][0].T)
    w["conv_w"] = np.ascontiguousarray(inp["c_conv_w"][0].reshape(4, 8, 128).transpose(2, 1, 0)).reshape(128, 32)
    w["conv_b"] = np.ascontiguousarray(inp["c_conv_b"][0].reshape(8, 128).T)
    w["bif"] = np.concatenate([inp["c_b_i"][0], inp["c_b_f"][0]]).astype(np.float32)
    w["head_g"] = np.ascontiguousarray(inp["c_head_g"][0])
    return {k: np.ascontiguousarray(v, dtype=np.float32) for k, v in w.items()}


def _phase_ATT(B):
    P = B.P
    S = B.S
    NTL = S // 128
    NB = S // 256
    NQG = S // 512
    with ExitStack() as es:
        B.es = es
        B.pfx = "at_"
        VA = B.sb("VA", [128, NTL, 8, 65], BF16)
        VA_tk = [B.tk() for _ in range(NTL)]
        KT = [B.sbuf("KT%d" % i, [128, S], BF16) for i in range(2)]
        KT_sem = [B.dsem("KT%d" % i) for i in range(2)]
        QA = [B.sbuf("QA%d" % i, [128, 512], BF16) for i in range(2)]
        QA_sem = [B.dsem("QA%d" % i) for i in range(2)]
        vst = [B.sbuf("vst%d" % i, [128, 512], BF16) for i in range(3)]
        vst_sem = [B.dsem("vst%d" % i) for i in range(3)]
        pts = [B.sbuf("pts%d" % i, [128, 512], BF16) for i in range(3)]
        dmb = B.sbuf("dmb", [128, 4, 512], BF16)
        kms = B.sbuf("kms", [64, 32], F32)
        kmb = [B.sbuf("kmb%d" % i, [64, 32], BF16) for i in range(2)]
        gbuf = B.sbuf("gbuf", [128, 32], F32)
        m8 = [B.sbuf("m8_%d" % i, [128, 8], F32) for i in range(2)]
        osb = [B.sbuf("osb%d" % i, [65, 512], F32) for i in range(2)]
        rden = [B.sbuf("rden%d" % i, [65, 1024], F32) for i in range(2)]
        rdh = [B.sbuf("rdh%d" % i, [65, 1024], BF16) for i in range(2)]
        onesb = B.sbuf("onesb", [65, 64], BF16)
        aT = [B.sbuf("aT%d" % i, [64, 512], BF16) for i in range(2)]
        identf = B.sbuf("identf", [128, 128], F32)
        ident = B.sbuf("ident", [128, 128], BF16)
        psS = [B.pbuf("psS%d" % i, [128, 512], F32) for i in range(3)]
        psO = [B.pbuf("psO%d" % i, [65, 512], F32) for i in range(2)]
        psB = B.pbuf("psB", [64, 512], F32)
        pgt = B.pbuf("pgt", [128, 128], F32)
        pgr = [Buf(None, B.tk()) for _ in range(4)]
        ptm = B.pbuf("ptm", [32, 1024], BF16)
        st_sem = [B.dsem("ast%d" % i) for i in range(2)]
        cs = B.dsem("acst")
        B.dma("sp", identf.t[:], B.dt["ident"].ap(), r=[], w=[identf], sem=cs)
        P.op("dve", lambda e: e.tensor_copy(out=ident.t[:], in_=identf.t[:]), r=[identf], w=[ident])
        P.op("dve", lambda e: e.memset(onesb.t[:], 1.0), r=[], w=[onesb])
        cs2 = B.dsem("acst2")
        B.dma("pool", dmb.t[:], B.dt["dmask"].ap().rearrange("r p n -> p r n"), r=[], w=[dmb], sem=cs2)
        for i in range(2):
            B.dma("pool", KT[i].t[64:96, :], B.dt["onehot"].ap(), r=[], w=[KT[i]], sem=cs2, grp=True)
        P.op("pool", lambda e: e.memset(VA[:, :, :, 64:65], 1.0), r=[], w=VA_tk)
        for t in range(NTL):
            vs = vst[t % 3]
            B.dma("sp", vs.t[:], B.dt["v_d"].ap()[t * 128:(t + 1) * 128, :], r=[B.dk("v_d", t * 128 // TG)],
                  w=[vs], sem=vst_sem[t % 3])
            P.op("pool", lambda e, t=t, vs=vs: e.tensor_copy(
                out=VA[:, t, :, 0:64], in_=vs.t[:].rearrange("p (h d) -> p h d", h=8)),
                r=[vs], w=[VA_tk[t]])
        mbt = [B.sbuf("mbq%d" % i, [128, 32], BF16) for i in range(8)]
        st_ = {"pi": 0, "si": 0}
        items = [(h, G) for h in range(8) for G in range(NQG)]
        kmh = {}

        def head_setup(h):
            KTs = KT[h % 2]
            B.dma("sp", KTs.t[0:64, :], B.dt["kT_d"].ap()[h * 64:(h + 1) * 64, :],
                  r=[B.dk("kT_d", g) for g in range(B.NG)], w=[KTs], sem=KT_sem[h % 2])
            km = kmb[h % 2]
            P.op("dve", lambda e, KTs=KTs: e.tensor_reduce(
                out=kms.t[:, 0:NB], in_=KTs.t[0:64, :].rearrange("p (n k) -> p n k", k=256),
                axis=AX.X, op=ALU.add), r=[KTs], w=[kms])
            P.op("act", lambda e, km=km: e.mul(out=km.t[:, 0:NB], in_=kms.t[:, 0:NB], mul=1.0 / 256.0),
                 r=[kms], w=[km])
            kmh[h] = km

        def qa_stage1(idx):
            h, G = items[idx]
            qa = QA[idx % 2]
            km = kmh[h]
            B.dma("sp", qa.t[0:64, :], B.dt["qT_d"].ap()[h * 64:(h + 1) * 64, G * 512:(G + 1) * 512],
                  r=[B.dk("qT_d", g) for g in range(B.NG)], w=[qa], sem=QA_sem[idx % 2])
            if G == 2 or (G == 0 and NQG <= 2):
                P.op("dve", lambda e: e.memset(gbuf.t[:], -1e30), r=[], w=[gbuf])
            for j in range(4):
                qb = 2 * G + j // 2
                mb = mbt[(idx % 2) * 4 + j]
                if qb >= 4:
                    P.op("pe", lambda e, qa=qa, j=j, km=km: e.matmul(
                        pgt.t[:, j * 32:j * 32 + NB], qa.t[0:64, j * 128:(j + 1) * 128], km.t[:, 0:NB],
                        start=True, stop=True), r=[qa, km], w=[pgr[j]])
                    P.op("dve", lambda e, qb=qb, j=j: e.tensor_copy(out=gbuf.t[:, 0:qb], in_=pgt.t[:, j * 32:j * 32 + qb]),
                         r=[pgr[j]], w=[gbuf])
                    m = m8[j % 2]
                    P.op("dve", lambda e, m=m, qb=qb: e.max(out=m.t[:], in_=gbuf.t[:, 0:max(qb, 8)]),
                         r=[gbuf], w=[m])
                    P.op("dve", lambda e, m=m, mb=mb: e.tensor_scalar(
                        out=mb.t[:], in0=gbuf.t[:], scalar1=m.t[:, 2:3], scalar2=NEGB,
                        op0=ALU.is_lt, op1=ALU.mult), r=[gbuf, m], w=[mb])
                    P.op("dve", lambda e, mb=mb, qb=qb: e.memset(mb.t[:, qb:qb + 1], 0.0), r=[], w=[mb])
                else:
                    P.op("dve", lambda e, mb=mb: e.memset(mb.t[:], NEGB), r=[], w=[mb])
                    P.op("dve", lambda e, mb=mb, qb=qb: e.memset(mb.t[:, 0:qb + 1], 0.0), r=[], w=[mb])

        def qa_stage2(idx):
            qa = QA[idx % 2]
            for j in range(4):
                mb = mbt[(idx % 2) * 4 + j]
                P.op("pe", lambda e, mb=mb, j=j: e.transpose(out=ptm.t[:, j * 128:(j + 1) * 128], in_=mb.t[:],
                                                             identity=ident.t[:]), r=[mb, ident], w=[ptm])
            P.op("dve", lambda e, qa=qa: e.tensor_copy(out=qa.t[64:96, :], in_=ptm.t[:, 0:512]), r=[ptm], w=[qa])

        def fin_stage1(idx):
            po = psO[idx % 2]
            ob = osb[idx % 2]
            rd = rden[idx % 2]
            rh = rdh[idx % 2]
            P.op("act", lambda e, ob=ob, po=po: e.copy(out=ob.t[:], in_=po.t[:]), r=[po], w=[ob])
            P.op("dve", lambda e, ob=ob, rd=rd: e.reciprocal(out=rd.t[64:65, 0:512], in_=ob.t[64:65, :]),
                 r=[ob], w=[rd])
            P.op("dve", lambda e, rd=rd, rh=rh: e.tensor_copy(out=rh.t[64:65, 0:512], in_=rd.t[64:65, 0:512]),
                 r=[rd], w=[rh])
            P.op("dve", lambda e, rd=rd, rh=rh: e.tensor_copy(out=rd.t[64:65, 512:1024], in_=rh.t[64:65, 0:512]),
                 r=[rh], w=[rd])
            P.op("dve", lambda e, rd=rd, rh=rh: e.tensor_tensor(out=rh.t[64:65, 512:1024], in0=rd.t[64:65, 0:512],
                                                                in1=rd.t[64:65, 512:1024], op=ALU.subtract),
                 r=[rd], w=[rh])

        def fin_stage2(idx):
            h, G = items[idx]
            ob = osb[idx % 2]
            rh = rdh[idx % 2]
            at = aT[idx % 2]
            for part in range(2):
                P.op("pe", lambda e, rh=rh, part=part: e.matmul(
                    psB.t[:], onesb.t[64:65, 0:64], rh.t[64:65, part * 512:(part + 1) * 512],
                    start=(part == 0), stop=(part == 1)), r=[onesb, rh], w=[psB])
            P.op("dve", lambda e, at=at, ob=ob: e.tensor_tensor(out=at.t[:], in0=ob.t[0:64, :], in1=psB.t[:],
                                                                op=ALU.mult), r=[ob, psB], w=[at])
            B.dma("pool", B.dt["yT_ab"].ap()[h * 64:(h + 1) * 64, G * 512:(G + 1) * 512], at.t[:],
                  r=[at], w=[B.dk("yT_ab", G // 2)], sem=st_sem[idx % 2])

        def s_mm(idx, kt):
            h, G = items[idx]
            KTs = KT[h % 2]
            qa = QA[idx % 2]
            pS = psS[st_["si"] % 3]
            st_["si"] += 1
            P.op("pe", lambda e, pS=pS, KTs=KTs, kt=kt, qa=qa: e.matmul(
                pS.t[:], KTs.t[0:96, kt * 128:(kt + 1) * 128], qa.t[0:96, :], start=True, stop=True),
                r=[KTs, qa], w=[pS])
            pt = pts[st_["pi"] % 3]
            st_["pi"] += 1
            P.op("act", lambda e, pt=pt, pS=pS: e.activation(out=pt.t[:], in_=pS.t[:], func=AF.Exp,
                                                             scale=0.125), r=[pS], w=[pt])
            if kt >= 4 * G:
                P.op("dve", lambda e, pt=pt, kt=kt, G=G: e.tensor_tensor(
                    out=pt.t[:], in0=pt.t[:], in1=dmb.t[:, kt - 4 * G, :], op=ALU.mult),
                    r=[pt, dmb], w=[pt])
            return pt

        head_setup(0)
        qa_stage1(0)
        qa_stage2(0)
        for idx, (h, G) in enumerate(items):
            po = psO[idx % 2]
            nkt = 4 * G + 4
            la = 2
            ptq = []
            for kt in range(min(la, nkt)):
                ptq.append(s_mm(idx, kt))
            for kt in range(nkt):
                if kt + la < nkt:
                    ptq.append(s_mm(idx, kt + la))
                pt = ptq[kt]
                P.op("pe", lambda e, po=po, kt=kt, h=h, pt=pt, nkt=nkt: e.matmul(
                    po.t[:], VA[:, kt, h, :], pt.t[:], start=(kt == 0), stop=(kt == nkt - 1)),
                    r=[Buf(None, VA_tk[kt]), pt], w=[po])
                if kt == 0:
                    if G == 0 and h + 1 < 8:
                        head_setup(h + 1)
                    if idx + 1 < len(items):
                        qa_stage1(idx + 1)
                if kt == 1 and idx > 0:
                    fin_stage2(idx - 1)
                if kt == max(nkt - 2, 2) and idx + 1 < len(items):
                    qa_stage2(idx + 1)
            fin_stage1(idx)
        fin_stage2(len(items) - 1)
        B.end_phase()


Builder.phase_ATT = _phase_ATT


def _outproj(B, g, yname, wname, wtk_name):
    P = B.P
    tok_g = g * TG
    xts = [Buf(None, k) for k in B.xT_tk]
    B.dma("sp", B.xT[:], B.dt[yname].ap()[:, tok_g:tok_g + TG].rearrange("(c p) n -> p c n", p=128),
          r=[B.dk(yname, g)], w=xts, sem=B.next_ld())
    for t in range(NT):
        B.load_res(t, "xr", tok_g + t * 128)
        rt = Buf(None, B.res_tk[t])
        P.op("act", lambda e, t=t: e.mul(out=B.res[:, t, :], in_=B.res[:, t, :], mul=ALPHA), r=[rt], w=[rt])
    wtk = B.dk(wtk_name)
    for nb in range(2):
        wp = B.load_wp(B.dt[wname][nb], 4096, wtk)
        for t in range(NT):
            pq = B.next_pa()
            for kc in range(8):
                P.op("pe", lambda e, pq=pq, kc=kc, t=t, wp=wp: e.matmul(
                    pq.t[:], B.xT[:, kc, t * 128:(t + 1) * 128], wp.t[:, kc * 512:(kc + 1) * 512],
                    start=(kc == 0), stop=(kc == 7)), r=[wp, xts[t]], w=[pq])
            rt = Buf(None, B.res_tk[t])
            P.op("dve", lambda e, pq=pq, t=t, nb=nb: e.tensor_tensor(
                out=B.res[:, t, nb * 512:(nb + 1) * 512], in0=pq.t[:], in1=B.res[:, t, nb * 512:(nb + 1) * 512],
                op=ALU.add), r=[pq, rt], w=[rt])


def _alloc_C(B):
    P = B.P
    B.pre = [B.sbuf("pre%d" % i, [128, 515], F32) for i in range(2)]
    B.acc = [B.sbuf("acc%d" % i, [128, 512], F32) for i in range(2)]
    B.halo = B.sbuf("halo", [128, 8, 3], F32)
    B.qko = [B.sbuf("qko%d" % i, [128, 512], BF16) for i in range(2)]
    B.vo = [B.sbuf("vo%d" % i, [128, 512], BF16) for i in range(3)]
    B.vo_i = 0
    B.convw = B.sbuf("convw", [128, 32], F32)
    B.convb = B.sbuf("convb", [128, 8], F32)
    B.wif = B.sbuf("wif", [128, 64], BF16)
    B.bifb = B.sbuf("bifb", [128, 8], F32)
    B.triuf = B.sbuf("triuf", [128, 128], F32)
    B.onesf = B.sbuf("onesf", [128, 128], F32)
    B.gt = [B.sbuf("gt%d" % i, [128, 16], F32) for i in range(2)]
    B.gw = [B.sbuf("gw%d" % i, [128, 24], F32) for i in range(2)]
    B.pgs = B.pd
    s = B.dsem("cstc")
    B.dma("sp", B.convw.t[:], B.dt["conv_w"].ap(), r=[], w=[B.convw], sem=s)
    B.dma("sp", B.convb.t[:], B.dt["conv_b"].ap(), r=[], w=[B.convb], sem=s, grp=True)
    B.dma("sp", B.bifb.t[:], B.dt["bif"].ap().partition_broadcast(128), r=[], w=[B.bifb], sem=s, grp=True)
    B.dma("sp", B.triuf.t[:], B.dt["triu"].ap(), r=[], w=[B.triuf], sem=s, grp=True)
    s2 = B.dsem("cstc2")
    B.dma("pool", B.wif.t[:], B.dt["wc_if"].ap(), r=[], w=[B.wif], sem=s2)
    P.op("dve", lambda e: e.memset(B.onesf.t[:], 1.0), r=[], w=[B.onesf])
    B.triub = B.sbuf("triub", [128, 128], BF16)
    B.onesb = B.sbuf("onesb", [128, 128], BF16)
    B.gh = [B.sbuf("gh%d" % i, [128, 8], BF16) for i in range(2)]
    P.op("dve", lambda e: e.tensor_copy(out=B.triub.t[:], in_=B.triuf.t[:]), r=[B.triuf], w=[B.triub])
    P.op("dve", lambda e: e.memset(B.onesb.t[:], 1.0), r=[], w=[B.onesb])
    P.op("dve", lambda e: e.memset(B.halo.t[:], 0.0), r=[], w=[B.halo])
    B.lnh = B.sbuf("lnh", [128, 1], F32)
    P.op("dve", lambda e: e.memset(B.lnh.t[:], -0.5 * float(np.log(128.0))), r=[], w=[B.lnh])


def _inproj_C(B, g):
    P = B.P
    tok_g = g * TG
    wtk = B.dk("wc_in")
    xts = [Buf(None, k) for k in B.xT_tk]
    ci = 0
    parts = os.environ.get("INC_PARTS", "abc")
    for pc in (range(2) if "a" in parts else ()):
        wp = B.load_wp(B.dt["wc_in_b"][pc], 4096, wtk)
        for oc in range(4):
            ch = pc * 4 + oc
            for hf in range(2):
                pq = B.next_pa()
                for kc in range(8):
                    P.op("pe", lambda e, pq=pq, kc=kc, oc=oc, hf=hf, wp=wp: e.matmul(
                        pq.t[:], wp.t[:, kc * 512 + oc * 128: kc * 512 + (oc + 1) * 128],
                        B.xT[:, kc, hf * 512:(hf + 1) * 512], start=(kc == 0), stop=(kc == 7)),
                        r=[wp] + xts[hf * 4:(hf + 1) * 4], w=[pq])
                pre = B.pre[ci % 2]
                acc = B.acc[ci % 2]
                qko = B.qko[ci % 2]
                ci += 1
                P.op("dve", lambda e, pre=pre, ch=ch: e.tensor_copy(out=pre.t[:, 0:3], in_=B.halo.t[:, ch, :]),
                     r=[B.halo], w=[pre])
                P.op("act", lambda e, pre=pre, pq=pq: e.copy(out=pre.t[:, 3:515], in_=pq.t[:]), r=[pq], w=[pre])
                P.op("dve", lambda e, pre=pre, ch=ch: e.tensor_copy(out=B.halo.t[:, ch, :], in_=pre.t[:, 512:515]),
                     r=[pre], w=[B.halo])
                P.op("dve", lambda e, pre=pre, acc=acc, ch=ch: e.tensor_scalar(
                    out=acc.t[:], in0=pre.t[:, 0:512], scalar1=B.convw.t[:, ch * 4:ch * 4 + 1], scalar2=None,
                    op0=ALU.mult), r=[pre, B.convw], w=[acc])
                for i in (1, 2, 3):
                    P.op("dve", lambda e, pre=pre, acc=acc, ch=ch, i=i: e.scalar_tensor_tensor(
                        out=acc.t[:], in0=pre.t[:, i:i + 512], scalar=B.convw.t[:, ch * 4 + i:ch * 4 + i + 1],
                        in1=acc.t[:], op0=ALU.mult, op1=ALU.add), r=[pre, acc, B.convw], w=[acc])
                P.op("act", lambda e, acc=acc, qko=qko, ch=ch: e.activation(
                    out=qko.t[:], in_=acc.t[:], func=AF.Silu, bias=B.convb.t[:, ch:ch + 1], scale=1.0),
                    r=[acc, B.convb], w=[qko])
                B.dma("pool", B.dt["qkT_c"].ap()[ch * 128:(ch + 1) * 128, tok_g + hf * 512: tok_g + (hf + 1) * 512],
                      qko.t[:], r=[qko], w=[B.dk("qkT_c", g)], sem=B.next_st())
    for pc in (range(2, 6) if "b" in parts else ()):
        wp = B.load_wp(B.dt["wc_in_b"][pc], 4096, wtk)
        for t in range(NT):
            tok0 = tok_g + t * 128
            pq = B.next_pa()
            for kc in range(8):
                P.op("pe", lambda e, pq=pq, kc=kc, t=t, wp=wp: e.matmul(
                    pq.t[:], B.xT[:, kc, t * 128:(t + 1) * 128], wp.t[:, kc * 512:(kc + 1) * 512],
                    start=(kc == 0), stop=(kc == 7)), r=[wp, xts[t]], w=[pq])
            vo = B.vo[B.vo_i % 3]
            B.vo_i += 1
            if pc < 4:
                P.op("act", lambda e, vo=vo, pq=pq: e.copy(out=vo.t[:], in_=pq.t[:]), r=[pq], w=[vo])
                dn, col = "v_c", (pc - 2) * 512
            else:
                P.op("act", lambda e, vo=vo, pq=pq: e.activation(out=vo.t[:], in_=pq.t[:], func=AF.Sigmoid),
                     r=[pq], w=[vo])
                dn, col = "o_c", (pc - 4) * 512
            B.dma("pool", B.dt[dn].ap()[tok0:tok0 + 128, col:col + 512], vo.t[:], r=[vo], w=[B.dk(dn, g)],
                  sem=B.next_st())
    for t in (range(NT) if "c" in parts else ()):
        tok0 = tok_g + t * 128
        pg = B.pgs[t % 2]
        gw = B.gw[t % 2]
        gt = B.gt[t % 2]
        for kc in range(8):
            P.op("pe", lambda e, pg=pg, kc=kc, t=t: e.matmul(
                pg.t[:, 0:8], B.xT[:, kc, t * 128:(t + 1) * 128], B.wif.t[:, kc * 8:(kc + 1) * 8],
                start=(kc == 0), stop=(kc == 7)), r=[B.wif, xts[t]], w=[pg])
        P.op("dve", lambda e, pg=pg, gw=gw: e.tensor_tensor(out=gw.t[:, 0:8], in0=pg.t[:, 0:8], in1=B.bifb.t[:],
                                                            op=ALU.add), r=[pg, B.bifb], w=[gw])
        GS = int(os.environ.get("GATE_STEPS", "9"))
        if GS < 2:
            continue
        P.op("act", lambda e, gw=gw: e.activation(out=gw.t[:, 8:12], in_=gw.t[:, 4:8], func=AF.Exp, scale=-1.0),
             r=[gw], w=[gw])
        P.op("act", lambda e, gw=gw: e.activation(out=gw.t[:, 8:12], in_=gw.t[:, 8:12], func=AF.Ln, bias=B.one1.t[:, 0:1],
                                                  scale=1.0), r=[gw, B.one1], w=[gw])
        if GS < 3:
            continue
        gh = B.gh[t % 2]
        P.op("dve", lambda e, gw=gw, gh=gh: e.tensor_copy(out=gh.t[:, 0:4], in_=gw.t[:, 8:12]), r=[gw], w=[gh])
        P.op("dve", lambda e, gw=gw, gh=gh: e.tensor_copy(out=gw.t[:, 20:24], in_=gh.t[:, 0:4]), r=[gh], w=[gw])
        P.op("dve", lambda e, gw=gw, gh=gh: e.tensor_tensor(out=gh.t[:, 4:8], in0=gw.t[:, 8:12], in1=gw.t[:, 20:24],
                                                            op=ALU.subtract), r=[gw], w=[gh])
        for (col, lh) in ((16, B.triub), (32, B.onesb)):
            for part in range(2):
                P.op("pe", lambda e, pg=pg, gh=gh, col=col, lh=lh, part=part: e.matmul(
                    pg.t[:, col:col + 4], lh.t[:], gh.t[:, part * 4:part * 4 + 4], start=(part == 0), stop=(part == 1)),
                    r=[lh, gh], w=[pg])
        if GS < 4:
            continue
        P.op("act", lambda e, pg=pg, gt=gt: e.activation(out=gt.t[:, 0:4], in_=pg.t[:, 16:20], func=AF.Exp, scale=-1.0),
             r=[pg], w=[gt])
        P.op("act", lambda e, pg=pg, gt=gt: e.activation(out=gt.t[:, 12:16], in_=pg.t[:, 32:36], func=AF.Exp, scale=-1.0),
             r=[pg], w=[gt])
        if GS < 5:
            continue
        P.op("dve", lambda e, pg=pg, gw=gw: e.tensor_tensor(out=gw.t[:, 12:16], in0=pg.t[:, 16:20], in1=gw.t[:, 0:4],
                                                            op=ALU.add), r=[pg, gw], w=[gw])
        P.op("dve", lambda e, pg=pg, gw=gw: e.tensor_tensor(out=gw.t[:, 16:20], in0=gw.t[:, 12:16], in1=pg.t[:, 32:36],
                                                            op=ALU.subtract), r=[pg, gw], w=[gw])
        P.op("act", lambda e, gw=gw, gt=gt: e.activation(out=gt.t[:, 4:12], in_=gw.t[:, 12:20], func=AF.Exp,
                                                         bias=B.lnh.t[:, 0:1], scale=1.0), r=[gw, B.lnh], w=[gt])
        B.dma("pool", B.dt["gates"].ap()[tok0:tok0 + 128, :], gt.t[:], r=[gt], w=[B.dk("gates", g)],
              sem=B.next_st())


def _phase_R1(B):
    with ExitStack() as es:
        B.es = es
        B.pfx = "r1_"
        B.alloc_R(1)
        B.alloc_consts_R()
        _alloc_C(B)
        B.one1 = B.sbuf("one1", [128, 1], F32)
        B.P.op("dve", lambda e: e.memset(B.one1.t[:], 1.0), r=[], w=[B.one1])
        for g in range(B.NG):
            tok_g = g * TG
            _outproj(B, g, "yT_ab", "wab_out_b", "wab_out")
            lnp = B.load_lnp(1)
            for t in range(NT):
                B.layer_norm(t, lnp)
                B.make_xT(t)
            for (f, li) in ((1, 2), (2, 3)):
                lnp = B.load_lnp(li)
                if not os.environ.get("SKIP_FFN"):
                    B.ffn(f)
                for t in range(NT):
                    B.layer_norm(t, lnp)
                    if f == 2:
                        B.store_res(t, "xr", tok_g + t * 128)
                    B.make_xT(t, scale_res=(f == 1))
            if not os.environ.get("SKIP_INC"):
                _inproj_C(B, g)
        B.end_phase()


Builder.phase_R1 = _phase_R1


def _phase_ML(B):
    P = B.P
    S = B.S
    NCH = S // 128
    with ExitStack() as es:
        B.es = es
        B.pfx = "ml_"
        NSL = 3
        qT = [B.sbuf("qT%d" % i, [128, 4, 128], BF16) for i in range(NSL)]
        kT = [B.sbuf("kT%d" % i, [128, 4, 128], BF16) for i in range(NSL)]
        va = [B.sbuf("va%d" % i, [128, 4, 258], BF16) for i in range(NSL)]
        vld = [B.sbuf("vld%d" % i, [128, 1024], BF16) for i in range(NSL)]
        og = [B.sbuf("og%d" % i, [128, 1024], BF16) for i in range(NSL)]
        gt = [B.sbuf("gt%d" % i, [128, 16], F32) for i in range(NSL)]
        ld_sem = [B.dsem("mld%d" % i) for i in range(NSL)]
        AT = [B.sbuf("AT%d" % i, [128, 128], BF16) for i in range(2)]
        kw = [B.sbuf("kw%d" % i, [128, 128], BF16) for i in range(2)]
        Cst = [B.sbuf("Cst%d" % h, [128, 258], F32) for h in range(4)]
        Cbf = [[B.sbuf("Cbf%d_%d" % (h, i), [128, 258], BF16) for i in range(2)] for h in range(4)]
        hb = [B.sbuf("hb%d" % i, [128, 1024], F32) for i in range(2)]
        hy = [B.sbuf("hy%d" % i, [128, 1024], BF16) for i in range(2)]
        sm = [B.sbuf("sm%d" % i, [128, 8], F32) for i in range(4)]
        stt = [B.sbuf("stt%d" % i, [128, 64], F32) for i in range(2)]
        yT = [B.sbuf("yT%d" % i, [128, 8, 128], BF16) for i in range(2)]
        hgb = B.sbuf("hgb", [128, 1024], F32)
        tri = B.sbuf("tri", [128, 128], F32)
        identf = B.sbuf("identf", [128, 128], F32)
        ident = B.sbuf("ident", [128, 128], BF16)
        eps = B.sbuf("eps", [128, 1], F32)
        pS = [B.pbuf("pS%d" % i, [128, 512], F32) for i in range(2)]
        pN = [B.pbuf("pN%d" % i, [128, 512], F32) for i in range(2)]
        pK = [B.pbuf("pK%d" % i, [128, 1024], BF16) for i in range(1)]
        pC = [B.pbuf("pC%d" % i, [128, 512], F32) for i in range(2)]
        pT = B.pbuf("pT", [128, 1024], BF16)
        pKb = [Buf(None, B.tk()) for _ in range(2)]
        st_sem = [B.dsem("mst%d" % i) for i in range(2)]
        cs = B.dsem("mcst")
        B.dma("sp", identf.t[:], B.dt["ident"].ap(), r=[], w=[identf], sem=cs)
        B.dma("sp", tri.t[:], B.dt["triu"].ap(), r=[], w=[tri], sem=cs, grp=True)
        B.dma("sp", hgb.t[:], B.dt["head_g"].ap().partition_broadcast(128), r=[], w=[hgb], sem=cs, grp=True)
        P.op("dve", lambda e: e.tensor_copy(out=ident.t[:], in_=identf.t[:]), r=[identf], w=[ident])
        P.op("dve", lambda e: e.memset(eps.t[:], LN_EPS), r=[], w=[eps])
        for h in range(4):
            P.op("dve", lambda e, h=h: e.memset(Cst[h].t[:], 0.0), r=[], w=[Cst[h]])
            P.op("dve", lambda e, h=h: e.memset(Cbf[h][0].t[:], 0.0), r=[], w=[Cbf[h][0]])
        for i in range(NSL):
            P.op("dve", lambda e, i=i: e.memset(va[i].t[:, :, 256:258], 1.0), r=[], w=[va[i]])
        ai = 0
        smi = 0
        for c in range(NCH):
            sl = c % NSL
            tok0 = c * 128
            g = tok0 // TG
            q_, k_, v_, o_, g_ = qT[sl], kT[sl], vld[sl], og[sl], gt[sl]
            sem = ld_sem[sl]
            B.dma("sp", q_.t[:], B.dt["qkT_c"].ap()[0:512, tok0:tok0 + 128].rearrange("(h p) n -> p h n", p=128),
                  r=[B.dk("qkT_c", g)], w=[q_], sem=sem)
            B.dma("sp", k_.t[:], B.dt["qkT_c"].ap()[512:1024, tok0:tok0 + 128].rearrange("(h p) n -> p h n", p=128),
                  r=[B.dk("qkT_c", g)], w=[k_], sem=sem, grp=True)
            B.dma("sp", v_.t[:], B.dt["v_c"].ap()[tok0:tok0 + 128, :], r=[B.dk("v_c", g)], w=[v_], sem=sem, grp=True)
            B.dma("sp", o_.t[:], B.dt["o_c"].ap()[tok0:tok0 + 128, :], r=[B.dk("o_c", g)], w=[o_], sem=sem, grp=True)
            B.dma("sp", g_.t[:], B.dt["gates"].ap()[tok0:tok0 + 128, :], r=[B.dk("gates", g)], w=[g_], sem=sem, grp=True)
            vv = va[sl]
            P.op("pool", lambda e, vv=vv, v_=v_: e.tensor_copy(
                out=vv.t[:, :, 0:256], in_=v_.t[:].rearrange("p (h d) -> p h d", h=4)), r=[v_], w=[vv])
            hbuf = hb[c % 2]
            for hp in range(2):
                hs = (2 * hp, 2 * hp + 1)
                cins = [Cbf[h][c % 2] for h in hs]
                couts = [Cbf[h][(c + 1) % 2] for h in hs]
                pss = [pS[i] for i in range(2)]
                pns = [pN[i] for i in range(2)]
                pcs = [pC[i] for i in range(2)]
                ats = [AT[i] for i in range(2)]
                kks = [kw[i] for i in range(2)]
                s4s = []
                for i in range(2):
                    s4s.append(sm[smi % 4])
                    smi += 1
                for i, h in enumerate(hs):
                    P.op("pe", lambda e, ps=pss[i], k_=k_, q_=q_, h=h: e.matmul(
                        ps.t[:, 0:128], k_.t[:, h, :], q_.t[:, h, :], start=True, stop=True), r=[k_, q_], w=[pss[i]])
                for i, h in enumerate(hs):
                    P.op("pe", lambda e, k_=k_, h=h, i=i: e.transpose(out=pK[0].t[:, i * 128:(i + 1) * 128],
                                                                      in_=k_.t[:, h, :], identity=ident.t[:]),
                         r=[k_, ident], w=[pKb[i]])
                for i, h in enumerate(hs):
                    P.op("dve", lambda e, ps=pss[i], at=ats[i], g_=g_, h=h: e.scalar_tensor_tensor(
                        out=at.t[:], in0=ps.t[:, 0:128], scalar=g_.t[:, 4 + h:5 + h], in1=tri.t[:],
                        op0=ALU.mult, op1=ALU.mult), r=[pss[i], g_, tri], w=[ats[i]])
                for i, h in enumerate(hs):
                    P.op("dve", lambda e, kk=kks[i], g_=g_, h=h, i=i: e.tensor_scalar(
                        out=kk.t[:], in0=pK[0].t[:, i * 128:(i + 1) * 128], scalar1=g_.t[:, 8 + h:9 + h],
                        scalar2=None, op0=ALU.mult), r=[pKb[i], g_], w=[kks[i]])
                for i, h in enumerate(hs):
                    P.op("pe", lambda e, pn=pns[i], at=ats[i], vv=vv, h=h: e.matmul(
                        pn.t[:, 0:257], at.t[:], vv.t[:, h, 0:257], start=True, stop=False), r=[ats[i], vv], w=[pns[i]])
                    P.op("pe", lambda e, pn=pns[i], q_=q_, cin=cins[i], h=h: e.matmul(
                        pn.t[:, 0:257], q_.t[:, h, :], cin.t[:, 0:257], start=False, stop=True),
                        r=[q_, cins[i]], w=[pns[i]])
                for i, h in enumerate(hs):
                    P.op("pe", lambda e, pc=pcs[i], kk=kks[i], vv=vv, h=h: e.matmul(
                        pc.t[:, 0:257], kk.t[:], vv.t[:, h, 0:257], start=True, stop=True), r=[kks[i], vv], w=[pcs[i]])
                for i, h in enumerate(hs):
                    P.op("dve", lambda e, s4=s4s[i], pn=pns[i], g_=g_, h=h: e.tensor_tensor(
                        out=s4.t[:, 0:1], in0=pn.t[:, 256:257], in1=g_.t[:, h:h + 1], op=ALU.mult),
                        r=[pns[i], g_], w=[s4s[i]])
                for i, h in enumerate(hs):
                    P.op("dve", lambda e, s4=s4s[i]: e.scalar_tensor_tensor(
                        out=s4.t[:, 4:5], in0=s4.t[:, 0:1], scalar=-1.0, in1=s4.t[:, 0:1], op0=ALU.mult, op1=ALU.max),
                        r=[s4s[i]], w=[s4s[i]])
                for i, h in enumerate(hs):
                    P.op("dve", lambda e, s4=s4s[i]: e.tensor_scalar(out=s4.t[:, 1:2], in0=s4.t[:, 4:5], scalar1=1.0,
                                                                     scalar2=None, op0=ALU.max), r=[s4s[i]], w=[s4s[i]])
                for i, h in enumerate(hs):
                    P.op("dve", lambda e, s4=s4s[i]: e.reciprocal(out=s4.t[:, 2:3], in_=s4.t[:, 1:2]),
                         r=[s4s[i]], w=[s4s[i]])
                for i, h in enumerate(hs):
                    P.op("dve", lambda e, s4=s4s[i], g_=g_, h=h: e.tensor_tensor(
                        out=s4.t[:, 3:4], in0=s4.t[:, 2:3], in1=g_.t[:, h:h + 1], op=ALU.mult),
                        r=[s4s[i], g_], w=[s4s[i]])
                for i, h in enumerate(hs):
                    P.op("act", lambda e, hbuf=hbuf, pn=pns[i], s4=s4s[i], h=h: e.activation(
                        out=hbuf.t[:, h * 256:(h + 1) * 256], in_=pn.t[:, 0:256], func=AF.Copy, scale=s4.t[:, 3:4]),
                        r=[pns[i], s4s[i]], w=[hbuf])
                for i, h in enumerate(hs):
                    P.op("dve", lambda e, pc=pcs[i], g_=g_, h=h: e.scalar_tensor_tensor(
                        out=Cst[h].t[:, 0:257], in0=Cst[h].t[:, 0:257], scalar=g_.t[:, 12 + h:13 + h],
                        in1=pc.t[:, 0:257], op0=ALU.mult, op1=ALU.add), r=[pcs[i], g_, Cst[h]], w=[Cst[h]])
                for i, h in enumerate(hs):
                    P.op("act", lambda e, cout=couts[i], h=h: e.copy(out=cout.t[:, 0:257], in_=Cst[h].t[:, 0:257]),
                         r=[Cst[h]], w=[couts[i]])
            if B.dbg:
                B.dma("pool", B.dt["dbg_h"].ap()[tok0:tok0 + 128, :], hbuf.t[:], r=[hbuf], w=[B.dk("dbg_h")],
                      sem=B.dsem("dbgh"))
            st = stt[c % 2]
            for h in range(4):
                P.op("dve", lambda e, st=st, hbuf=hbuf, h=h: e.bn_stats(out=st.t[:, 6 * h:6 * h + 6],
                                                                        in_=hbuf.t[:, h * 256:(h + 1) * 256]),
                     r=[hbuf], w=[st])
                P.op("dve", lambda e, st=st, h=h: e.bn_aggr(out=st.t[:, 24 + 2 * h:26 + 2 * h], in_=st.t[:, 6 * h:6 * h + 6]),
                     r=[st], w=[st])
            P.op("act", lambda e, st=st: e.activation(
                out=st.t[:, 32:36], in_=st.t[:, 25:32:2],
                func=AF.Sqrt, bias=eps.t[:, 0:1], scale=1.0), r=[st, eps], w=[st])
            P.op("dve", lambda e, st=st: e.reciprocal(out=st.t[:, 36:40], in_=st.t[:, 32:36]), r=[st], w=[st])
            for h in range(4):
                P.op("dve", lambda e, st=st, hbuf=hbuf, h=h: e.tensor_scalar(
                    out=hbuf.t[:, h * 256:(h + 1) * 256], in0=hbuf.t[:, h * 256:(h + 1) * 256],
                    scalar1=st.t[:, 24 + 2 * h:25 + 2 * h], scalar2=st.t[:, 36 + h:37 + h],
                    op0=ALU.subtract, op1=ALU.mult), r=[st, hbuf], w=[hbuf])
            P.op("pool", lambda e, hbuf=hbuf: e.tensor_tensor(out=hbuf.t[:], in0=hbuf.t[:], in1=hgb.t[:], op=ALU.mult),
                 r=[hbuf, hgb], w=[hbuf])
            yb = hy[c % 2]
            P.op("pool", lambda e, hbuf=hbuf, yb=yb, o_=o_: e.tensor_tensor(out=yb.t[:], in0=hbuf.t[:], in1=o_.t[:],
                                                                            op=ALU.mult), r=[hbuf, o_], w=[yb])
            for kc in range(8):
                P.op("pe", lambda e, kc=kc, yb=yb: e.transpose(out=pT.t[:, kc * 128:(kc + 1) * 128],
                                                               in_=yb.t[:, kc * 128:(kc + 1) * 128],
                                                               identity=ident.t[:]), r=[yb, ident], w=[pT])
            yt = yT[c % 2]
            P.op("act", lambda e, yt=yt: e.copy(out=yt.t[:], in_=pT.t[:].rearrange("p (k n) -> p k n", k=8)),
                 r=[pT], w=[yt])
            B.dma("pool", B.dt["yT_c"].ap()[:, tok0:tok0 + 128].rearrange("(c p) n -> p c n", p=128), yt.t[:],
                  r=[yt], w=[B.dk("yT_c", g)], sem=st_sem[c % 2])
        B.end_phase()


def _phase_R2(B):
    with ExitStack() as es:
        B.es = es
        B.pfx = "r2_"
        B.alloc_R(2)
        B.alloc_consts_R()
        for g in range(B.NG):
            tok_g = g * TG
            _outproj(B, g, "yT_c", "wc_out_b", "wc_out")
            lnp = B.load_lnp(4)
            for t in range(NT):
                B.layer_norm(t, lnp)
                B.make_xT(t)
            lnp = B.load_lnp(5)
            B.ffn(3)
            for t in range(NT):
                B.layer_norm(t, lnp)
                B.store_res(t, "y", tok_g + t * 128)
        B.end_phase()


Builder.phase_ML = _phase_ML
Builder.phase_R2 = _phase_R2


_SEQ = 8192
_BATCH = 4
_PROG = {}


def _build(S):
    if S in _PROG:
        return _PROG[S]
    B = Builder(S, dbg=False)
    B.declare()
    B.phase_R0()
    B.phase_ATT()
    B.phase_R1()
    B.phase_ML()
    B.phase_R2()
    _PROG[S] = B
    return B


def kernel(**inputs):
    inp = {k: np.asarray(v) for k, v in inputs.items()}
    x = np.asarray(inp["x"], dtype=np.float32)
    Bn, S, _ = x.shape
    B = _build(S)
    w = prep_weights(inp)
    c = _consts(S)
    workers = [0, 1, 4, 5][:Bn] if Bn <= 4 else list(range(Bn))
    zw = {k: np.zeros_like(v) for k, v in w.items()}
    zc = {k: np.zeros_like(v) for k, v in c.items()}
    zx = np.zeros_like(x[0])
    in_maps = []
    for core in range(N_CORES):
        if core in workers:
            m = dict(w)
            m.update(c)
            m["x"] = np.ascontiguousarray(x[workers.index(core)])
        else:
            m = dict(zw)
            m.update(zc)
            m["x"] = zx
        in_maps.append(m)
    res = run_bass_kernel_spmd(B.nc, in_maps, core_ids=list(range(N_CORES)))
    out = np.stack([np.asarray(res.results[workers[b]]["y"], dtype=np.float32) for b in range(Bn)], axis=0)
    return out
```
